# Optimizing a Trainium2 kernel written in Bass

```python
import jax, jax.numpy as jnp
from jax import lax
import numpy as np

D_MODEL = 2048
BATCH = 2
SEQ = 16384
DEPTH = 1

D_MIX = D_MODEL
D_RWKV = D_MIX // 2
D_RET = D_MIX - D_RWKV
RWKV_HEAD = 64
RWKV_HEADS = D_RWKV // RWKV_HEAD
RET_HEADS = 4
RET_HEAD = D_RET // RET_HEADS
RET_CHUNK = 128
LORA_DECAY = 64
LORA_A = 64
LORA_GATE = 128
D_FF = 4 * D_MODEL
ROPE_BASE = 10000.0
LN_EPS = 1e-5
RWKV_GN_EPS = 64e-5
RET_GN_EPS = 1e-6
ALPHA = (2.0 * DEPTH) ** 0.25
BETA = (8.0 * DEPTH) ** -0.25

RWKV_SHIFT_WIDTH = 3 * D_RWKV + LORA_DECAY + LORA_A + LORA_GATE
N_IN = RWKV_SHIFT_WIDTH + 4 * D_RET
RWKV_SPLITS = (D_RWKV, 2 * D_RWKV, 3 * D_RWKV, 3 * D_RWKV + LORA_DECAY,
               3 * D_RWKV + LORA_DECAY + LORA_A)
RET_SPLITS = (D_RET, 2 * D_RET, 3 * D_RET)

kernel_name = "hymba_rwkv7_retnet_deepnorm"


def _layer_norm(x, w, b):
    xf = x.astype(jnp.float32)
    mu = xf.mean(-1, keepdims=True)
    var = jnp.square(xf - mu).mean(-1, keepdims=True)
    return ((xf - mu) * lax.rsqrt(var + LN_EPS) * w + b).astype(x.dtype)


def _head_norm(y, eps):
    mu = y.mean(-1, keepdims=True)
    var = jnp.square(y - mu).mean(-1, keepdims=True)
    return (y - mu) * lax.rsqrt(var + eps)


def _token_shift(f, mu):
    prev = jnp.pad(f, ((0, 0), (1, 0), (0, 0)))[:, :-1]
    return f + (prev - f) * mu


def _wkv7_scan(r, decay, k, v, a_vec, b_vec):
    B, T, H, N = r.shape

    def step(S, inp):
        r_t, w_t, k_t, v_t, a_t, b_t = inp
        sa = jnp.einsum('bhvk,bhk->bhv', S, a_t)
        S = (S * w_t[:, :, None, :] + sa[..., None] * b_t[:, :, None, :]
             + v_t[..., None] * k_t[:, :, None, :])
        y_t = jnp.einsum('bhvk,bhk->bhv', S, r_t)
        return S, y_t

    xs = tuple(jnp.moveaxis(t, 1, 0) for t in (r, decay, k, v, a_vec, b_vec))
    S0 = jnp.zeros((B, H, N, N), jnp.float32)
    _, y = lax.scan(step, S0, xs)
    return jnp.moveaxis(y, 0, 1)


def _rwkv7_mix(feat, mu_shift, w0, w_lora_up, a0, a_lora_up, g_lora_up,
               k_k, k_a, r_k, gn_w, gn_b):
    B, T, _ = feat.shape
    f = _token_shift(feat.astype(jnp.float32), mu_shift)
    r, k, v, lw, la, lg = jnp.split(f, RWKV_SPLITS, axis=-1)
    w = -jax.nn.softplus(-(w0 + jnp.tanh(lw) @ w_lora_up)) - 0.5
    a = jax.nn.sigmoid(a0 + la @ a_lora_up)
    g = jax.nn.sigmoid(lg) @ g_lora_up

    def heads(t):
        return t.reshape(B, T, RWKV_HEADS, RWKV_HEAD)

    kk = heads(k * k_k)
    kk = kk / jnp.maximum(jnp.sqrt(jnp.sum(kk * kk, -1, keepdims=True)), 1e-12)
    k = k * (1.0 + (a - 1.0) * k_a)
    r, k, v, a = heads(r), heads(k), heads(v), heads(a)
    decay = jnp.exp(-jnp.exp(heads(w)))
    y = _wkv7_scan(r, decay, k, v, -kk, kk * a)
    y = _head_norm(y, RWKV_GN_EPS) * gn_w.reshape(RWKV_HEADS, RWKV_HEAD) \
        + gn_b.reshape(RWKV_HEADS, RWKV_HEAD)
    bonus = jnp.sum(r * k * r_k, -1, keepdims=True) * v
    return (y + bonus).reshape(B, T, D_RWKV) * g


def _rotary(t, cos, sin):
    t1, t2 = t[..., 0::2], t[..., 1::2]
    return jnp.stack([t1 * cos - t2 * sin, t1 * sin + t2 * cos], axis=-1).reshape(t.shape)


def _retention_mix(q, k, v, g, gn_w):
    B, T, _ = q.shape
    nC = T // RET_CHUNK

    def heads(t):
        return t.astype(jnp.float32).reshape(B, T, RET_HEADS, RET_HEAD).transpose(0, 2, 1, 3)

    q, k, v = heads(q), heads(k), heads(v)
    pos = jnp.arange(T, dtype=jnp.float32)
    inv_freq = 1.0 / (ROPE_BASE ** jnp.linspace(0.0, 1.0, RET_HEAD // 2, dtype=jnp.float32))
    theta = pos[:, None] * inv_freq[None, :]
    cos, sin = jnp.cos(theta), jnp.sin(theta)
    q = _rotary(q, cos, sin)
    k = _rotary(k, cos, sin) * RET_HEAD ** -0.5

    log_gamma = jnp.log(1.0 - 2.0 ** (-5.0 - jnp.arange(RET_HEADS, dtype=jnp.float32)))
    idx = jnp.arange(RET_CHUNK, dtype=jnp.float32)
    rel = idx[:, None] - idx[None, :]
    decay_mask = jnp.where(rel >= 0,
                           jnp.exp(jnp.maximum(rel, 0.0) * log_gamma[:, None, None]), 0.0)
    q_decay = jnp.exp((idx + 1.0) * log_gamma[:, None])
    k_decay = jnp.exp((RET_CHUNK - 1.0 - idx) * log_gamma[:, None])
    chunk_decay = jnp.exp(RET_CHUNK * log_gamma)

    def chunks(t):
        return t.reshape(B, RET_HEADS, nC, RET_CHUNK, RET_HEAD)

    qc, kc, vc = chunks(q), chunks(k), chunks(v)
    scores = jnp.einsum('bhncd,bhnmd->bhncm', qc, kc) * decay_mask[:, None]
    intra = jnp.einsum('bhncm,bhnme->bhnce', scores, vc)

    def step(R, inp):
        q_n, k_n, v_n = inp
        cross = jnp.einsum('bhcd,bhde->bhce', q_n, R) * q_decay[None, :, :, None]
        R = R * chunk_decay[None, :, None, None] + jnp.einsum(
            'bhcd,bhce->bhde', k_n * k_decay[None, :, :, None], v_n)
        return R, cross

    xs = (jnp.moveaxis(qc, 2, 0), jnp.moveaxis(kc, 2, 0), jnp.moveaxis(vc, 2, 0))
    R0 = jnp.zeros((B, RET_HEADS, RET_HEAD, RET_HEAD), jnp.float32)
    _, cross = lax.scan(step, R0, xs)
    y = intra + jnp.moveaxis(cross, 0, 2)
    y = y.reshape(B, RET_HEADS, T, RET_HEAD).transpose(0, 2, 1, 3)
    y = _head_norm(y, RET_GN_EPS) * gn_w.reshape(RET_HEADS, RET_HEAD)
    return y.reshape(B, T, D_RET) * jax.nn.silu(g.astype(jnp.float32))


def _hybrid_layer(x, w_in, mu_shift, w0, w_lora_up, a0, a_lora_up, g_lora_up,
                  k_k, k_a, r_k, rwkv_gn_w, rwkv_gn_b, ret_gn_w, w_o,
                  ln1_w, ln1_b, w_up, w_down, ln2_w, ln2_b):
    h = x @ w_in
    y_a = _rwkv7_mix(h[..., :RWKV_SHIFT_WIDTH], mu_shift, w0, w_lora_up, a0,
                     a_lora_up, g_lora_up, k_k, k_a, r_k, rwkv_gn_w, rwkv_gn_b)
    q, k, v, g = jnp.split(h[..., RWKV_SHIFT_WIDTH:], RET_SPLITS, axis=-1)
    y_b = _retention_mix(q, k, v, g, ret_gn_w)
    mix = jnp.concatenate([y_a, y_b], axis=-1).astype(x.dtype) @ w_o
    x = _layer_norm(ALPHA * x + mix, ln1_w, ln1_b)
    ff = jnp.square(jax.nn.relu(x @ w_up)) @ w_down
    return _layer_norm(ALPHA * x + ff, ln2_w, ln2_b)


def setup_inputs(seed: int = 0) -> dict:
    key = jax.random.key(seed)
    ks = jax.random.split(key, 24)
    f32 = jnp.float32

    def nrm(k, shape, scale):
        return jax.random.normal(k, shape, f32) * scale

    L = DEPTH
    col_scale = jnp.concatenate([
        jnp.ones((2 * D_RWKV,), f32), jnp.full((D_RWKV,), BETA, f32),
        jnp.ones((LORA_DECAY + LORA_A + LORA_GATE + 2 * D_RET,), f32),
        jnp.full((D_RET,), BETA, f32), jnp.ones((D_RET,), f32)])
    ratio = jnp.arange(D_RWKV, dtype=f32) / (D_RWKV - 1)
    return {
        "x": nrm(ks[0], (BATCH, SEQ, D_MODEL), 1.0),
        "w_in": nrm(ks[1], (L, D_MODEL, N_IN), D_MODEL ** -0.5) * col_scale,
        "mu_shift": jax.random.uniform(ks[2], (L, RWKV_SHIFT_WIDTH), f32, 0.1, 0.9),
        "w0": jnp.broadcast_to(-6.5 + 5.0 * ratio ** 0.85, (L, D_RWKV)) + nrm(ks[3], (L, D_RWKV), 0.01),
        "w_lora_up": nrm(ks[4], (L, LORA_DECAY, D_RWKV), 0.1 * LORA_DECAY ** -0.5),
        "a0": nrm(ks[5], (L, D_RWKV), 0.01),
        "a_lora_up": nrm(ks[6], (L, LORA_A, D_RWKV), 0.5 * LORA_A ** -0.5),
        "g_lora_up": nrm(ks[7], (L, LORA_GATE, D_RWKV), LORA_GATE ** -0.5),
        "k_k": 0.85 + nrm(ks[8], (L, D_RWKV), 0.02),
        "k_a": 1.0 + nrm(ks[9], (L, D_RWKV), 0.02),
        "r_k": nrm(ks[10], (L, RWKV_HEADS, RWKV_HEAD), 0.1),
        "rwkv_gn_w": 1.0 + nrm(ks[11], (L, D_RWKV), 0.02),
        "rwkv_gn_b": nrm(ks[12], (L, D_RWKV), 0.02),
        "ret_gn_w": 1.0 + nrm(ks[13], (L, D_RET), 0.02),
        "w_o": nrm(ks[14], (L, D_MIX, D_MODEL), BETA * D_MIX ** -0.5),
        "ln1_w": 1.0 + nrm(ks[15], (L, D_MODEL), 0.02),
        "ln1_b": nrm(ks[16], (L, D_MODEL), 0.02),
        "w_up": nrm(ks[17], (L, D_MODEL, D_FF), BETA * D_MODEL ** -0.5),
        "w_down": nrm(ks[18], (L, D_FF, D_MODEL), BETA * D_FF ** -0.5),
        "ln2_w": 1.0 + nrm(ks[19], (L, D_MODEL), 0.02),
        "ln2_b": nrm(ks[20], (L, D_MODEL), 0.02),
    }


def reference(x, w_in, mu_shift, w0, w_lora_up, a0, a_lora_up, g_lora_up,
              k_k, k_a, r_k, rwkv_gn_w, rwkv_gn_b, ret_gn_w, w_o,
              ln1_w, ln1_b, w_up, w_down, ln2_w, ln2_b):
    for l in range(DEPTH):
        x = _hybrid_layer(x, w_in[l], mu_shift[l], w0[l], w_lora_up[l], a0[l],
                          a_lora_up[l], g_lora_up[l], k_k[l], k_a[l], r_k[l],
                          rwkv_gn_w[l], rwkv_gn_b[l], ret_gn_w[l], w_o[l],
                          ln1_w[l], ln1_b[l], w_up[l], w_down[l], ln2_w[l], ln2_b[l])
    return x
```

```python
import math
from contextlib import ExitStack

import numpy as np
import concourse.bass as bass
import concourse.mybir as mybir
from concourse.bass_utils import run_bass_kernel_spmd

F32 = mybir.dt.float32
BF16 = mybir.dt.bfloat16
AF = mybir.ActivationFunctionType
ALU = mybir.AluOpType
AX = mybir.AxisListType

D = 2048
DFF = 8192
SEQ = 16384
ALPHA = 2.0 ** 0.25
TB = 256
CH = 64
TT = 512
EPOCH = 20000
COMPUTE = ("pe", "act", "dve", "pool")


class Op:
    __slots__ = ("eng", "fn", "reads", "writes", "is_dma", "dkey", "dval", "dinc", "idx", "waits", "ms",
                 "need_ms")

    def __init__(self, eng, fn, reads, writes, is_dma=False, dkey=None):
        self.eng = eng
        self.fn = fn
        self.reads = reads
        self.writes = writes
        self.is_dma = is_dma
        self.dkey = dkey
        self.dval = 0
        self.dinc = 16
        self.idx = -1
        self.waits = []
        self.ms = -1
        self.need_ms = False


def _k(x):
    return getattr(x, "key", x)


class Prog:
    def __init__(self, nc, psum, nring=None):
        self.nc = nc
        self.nring = nring or len(psum)
        self.per_eng = {e: [] for e in ("pe", "act", "dve", "pool", "sync")}
        self.last_w = {}
        self.readers = {}
        self.dma_cnt = {}
        self.final_eng = "sync"
        self.psum = psum
        self.ps_i = 0

    def mark(self, name):
        import os
        if os.environ.get("KSTOP") == name:
            self.stopped = True

    def ps(self):
        b = self.psum[self.ps_i % self.nring]
        self.ps_i += 1
        return b

    def op(self, eng, fn, reads=(), writes=()):
        reads = [_k(x) for x in reads]
        writes = [_k(x) for x in writes]
        for r in reads:
            if isinstance(r, str) and r.startswith("ps") and r not in writes:
                writes.append(r)
        o = Op(eng, fn, tuple(reads), tuple(writes))
        self._add(o)
        return o

    def dma(self, eng, fn, key, reads=(), writes=(), inc=16):
        if getattr(self, "stopped", False):
            return None
        o = Op(eng, fn, tuple(_k(x) for x in reads), tuple(_k(x) for x in writes), True, key)
        self.dma_cnt[key] = self.dma_cnt.get(key, 0) + inc
        o.dval = self.dma_cnt[key]
        o.dinc = inc
        self._add(o)
        return o

    def _dep(self, o, y, kind):
        if y is None or y is o:
            return
        if y.is_dma:
            o.waits.append(("dma", y.dkey, y.dval))
            return
        if y.eng == o.eng:
            if not o.is_dma and y.eng == "pe":
                return
        y.need_ms = True
        o.waits.append(("ms", y.eng, y))

    def _add(self, o):
        if getattr(self, "stopped", False):
            return
        lst = self.per_eng[o.eng]
        o.idx = len(lst)
        for r in o.reads:
            self._dep(o, self.last_w.get(r), "raw")
        for w in o.writes:
            self._dep(o, self.last_w.get(w), "waw")
            for rd in self.readers.get(w, ()):
                self._dep(o, rd, "war")
        for r in o.reads:
            self.readers.setdefault(r, []).append(o)
        for w in o.writes:
            self.last_w[w] = o
            self.readers[w] = []
        lst.append(o)

    def emit(self, stack, tag):
        nc = self.nc
        n_ms = {e: 0 for e in COMPUTE}
        for e in COMPUTE:
            for o in self.per_eng[e]:
                if o.need_ms and not o.is_dma:
                    o.ms = n_ms[e]
                    n_ms[e] += 1
        sems = {}
        for e in COMPUTE:
            sems[e] = [stack.enter_context(nc.semaphore(f"{tag}m_{e}_{i}"))
                       for i in range(n_ms[e] // EPOCH + 1)]
        dsem = {k: stack.enter_context(nc.semaphore(f"{tag}d_{i}"))
                for i, k in enumerate(self.dma_cnt)}
        block = stack.enter_context(nc.Block())
        final_eng = self.final_eng

        def replay(ename, eng):
            waited_ms = {e: -1 for e in COMPUTE}
            waited_d = {}
            for o in self.per_eng[ename]:
                need_ms = {}
                need_d = {}
                for w in o.waits:
                    if w[0] == "ms":
                        y = w[2]
                        if y.ms > waited_ms[y.eng]:
                            need_ms[y.eng] = max(need_ms.get(y.eng, -1), y.ms)
                    else:
                        if w[2] > waited_d.get(w[1], 0):
                            need_d[w[1]] = max(need_d.get(w[1], 0), w[2])
                for e, m in need_ms.items():
                    eng.wait_ge(sems[e][m // EPOCH], m % EPOCH + 1)
                    waited_ms[e] = m
                for k, v in need_d.items():
                    eng.wait_ge(dsem[k], v)
                    waited_d[k] = v
                ins = o.fn(eng)
                if o.is_dma:
                    ins.then_inc(dsem[o.dkey], o.dinc)
                elif o.ms >= 0:
                    ins.then_inc(sems[o.eng][o.ms // EPOCH], 1)
            if ename == final_eng:
                for k, v in self.dma_cnt.items():
                    if v > waited_d.get(k, 0):
                        eng.wait_ge(dsem[k], v)

        @block.tensor
        def _(e):
            replay("pe", e)

        @block.scalar
        def _(e):
            replay("act", e)

        @block.vector
        def _(e):
            replay("dve", e)

        @block.gpsimd
        def _(e):
            replay("pool", e)

        @block.sync
        def _(e):
            replay("sync", e)


class Buf:
    def __init__(self, t, key):
        self.t = t
        self.key = key

    def __getitem__(self, idx):
        return self.t[idx]


class K:
    def __init__(self, P, nc, st):
        self.P, self.nc, self.st = P, nc, st

    def sb(self, name, shape, dt):
        import os
        lim = int(os.environ.get("KALLOCSTOP", "100000"))
        self.n_alloc = getattr(self, "n_alloc", 0) + 1
        if self.n_alloc > lim:
            return self.last_buf
        if self.n_alloc == lim:
            print("last alloc:", name, shape, dt)
        self.last_buf = Buf(self.st.enter_context(self.nc.sbuf_tensor("s_" + name, list(shape), dt)), name)
        return self.last_buf
        return Buf(self.st.enter_context(self.nc.sbuf_tensor("s_" + name, list(shape), dt)), name)

    def mm(self, out, lhsT, rhs, start, stop, rd, wr):
        self.P.op("pe", lambda e: e.matmul(out, lhsT=lhsT, rhs=rhs, start=start, stop=stop), rd, wr)

    def act(self, out, in_, func, rd, wr, bias=None, scale=None):
        kw = {}
        if bias is not None:
            kw["bias"] = bias
        if scale is not None:
            kw["scale"] = scale
        self.P.op("act", lambda e: e.activation(out=out, in_=in_, func=func, **kw), rd, wr)

    def tt(self, eng, out, in0, in1, op, rd, wr):
        self.P.op(eng, lambda e: e.tensor_tensor(out=out, in0=in0, in1=in1, op=op), rd, wr)

    def ts(self, eng, out, in0, s1, s2, op0, op1, rd, wr):
        if op1 is None:
            self.P.op(eng, lambda e: e.tensor_scalar(out=out, in0=in0, scalar1=s1, scalar2=None,
                                                     op0=op0), rd, wr)
        else:
            self.P.op(eng, lambda e: e.tensor_scalar(out=out, in0=in0, scalar1=s1, scalar2=s2,
                                                     op0=op0, op1=op1), rd, wr)

    def stt(self, out, in0, scalar, in1, op0, op1, rd, wr):
        self.P.op("dve", lambda e: e.scalar_tensor_tensor(out=out, in0=in0, scalar=scalar, in1=in1,
                                                          op0=op0, op1=op1), rd, wr)

    def cp(self, eng, out, in_, rd, wr):
        if eng == "act":
            self.P.op("act", lambda e: e.activation(out=out, in_=in_, func=AF.Copy), rd, wr)
        else:
            self.P.op(eng, lambda e: e.tensor_copy(out=out, in_=in_), rd, wr)

    def memset(self, eng, buf, ap, val):
        self.P.op(eng, lambda e: e.memset(ap, val), [], [buf])

    def dma(self, eng, out, in_, key, rd, wr):
        import os
        if key in os.environ.get("KSKIP", "").split(","):
            return
        self.P.dma(eng, lambda e: e.dma_start(out=out, in_=in_), key, rd, wr)


def phase1(nc, psum, T, dr, yrow0=0, tag="a"):
    TQ = T // 4
    NB = T // TB
    NC = TB // CH
    NR = TB // 128
    with ExitStack() as st:
        P = Prog(nc, psum, 6)
        P.final_eng = "pool"
        k = K(P, nc, st)
        sb = k.sb
        wfm = sb("wfm", [128, 16, 1536], BF16)
        wvg = sb("wvg", [128, 16, 512], BF16)
        xtbs = [sb(f"xtb{i}", [128, 16, TB], BF16) for i in range(2)]
        pv = sb("pv", [128, 24], F32)
        omm = sb("omm", [128, 8], F32)
        omka = sb("omka", [128, 2], F32)
        wupw = sb("wupw", [128, 256], BF16)
        wupa = sb("wupa", [128, 256], BF16)
        gup = sb("gup", [128, 256], BF16)
        mask512 = sb("mask512", [128, 512], F32)
        maskT = sb("maskT", [128, 128], F32)
        ident = sb("ident", [128, 128], BF16)
        esel = sb("esel", [128, 64], BF16)
        onesbd = sb("onesbd", [128, 128], BF16)
        ones = sb("ones", [128, 64], F32)
        retgn = sb("retgn", [128, 256], F32)
        rmask = sb("rmask", [128, 128], F32)
        rkdec = sb("rkdec", [128, 1], F32)
        rqdec = sb("rqdec", [128, 256], F32)

        xT_v = dr["xT"].rearrange("(k p) t -> p k t", p=128)
        k.dma("pool", wfm[:], dr["wfm"].rearrange("(k p) c -> p k c", p=128), "c_wfm", [], [wfm])
        k.dma("pool", wvg[:], dr["wvg"].rearrange("(k p) c -> p k c", p=128), "c_wvg", [], [wvg])
        k.dma("sync", pv[:], dr["pvec"], "c_pv", [], [pv])
        k.memset("pool", wupw, wupw[:], 0.0)
        k.memset("pool", wupa, wupa[:], 0.0)
        k.dma("pool", wupw[0:64, :], dr["wupw"], "c_wupw", [], [wupw])
        k.dma("pool", wupa[64:128, :], dr["wupa"], "c_wupa", [], [wupa])
        k.dma("pool", gup[:], dr["gup"], "c_gup", [], [gup])
        k.dma("sync", mask512[:], dr["mask512"], "c_m512", [], [mask512])
        k.dma("sync", maskT[:], dr["maskT"], "c_mT", [], [maskT])
        k.dma("pool", ident[:], dr["ident"], "c_id", [], [ident])
        k.dma("pool", esel[:], dr["esel"], "c_es", [], [esel])
        k.dma("pool", onesbd[:], dr["onesbd"], "c_ob", [], [onesbd])
        k.dma("sync", retgn[:], dr["retgn"], "c_rg", [], [retgn])
        k.dma("sync", rmask[:], dr["rmask"], "c_rm", [], [rmask])
        k.dma("sync", rkdec[:], dr["rkdec"], "c_rk", [], [rkdec])
        k.dma("sync", rqdec[:], dr["rqdec"], "c_rq", [], [rqdec])
        k.memset("dve", ones, ones[:], 1.0)
        k.ts("dve", omm[:], pv[:, 0:8], -1.0, 1.0, ALU.mult, ALU.add, [pv], [omm])
        k.ts("dve", omka[:], pv[:, 14:16], -1.0, 1.0, ALU.mult, ALU.add, [pv], [omka])

        P.mark('consts')
        if getattr(P, 'stopped', False):
            import os
            nb = int(os.environ.get("KDUMMY", "0"))
            if nb:
                dm = sb("dummy", [128, nb // 4], F32)
                k.memset("dve", dm, dm[:, 0:64], 0.0)
            for i_ in range(int(os.environ.get("KMANY", "0"))):
                sb(f"many{i_}", [128, 16], F32)
            print("remaining", nc.sbuf_bytes_remaining)
        hraw = [sb(f"hraw{i}", [128, TB + 1], F32) for i in range(8)]
        h1 = [sb(f"h1_{i}", [128, TB], F32) for i in range(2)]
        Ff = [sb(f"F{i}", [128, TB], F32) for i in range(8)]
        QK = [sb(f"QK{i}", [128, TB], F32) for i in range(4)]
        vtm = [sb(f"vtm{i}", [128, 256], BF16) for i in range(NR)]
        sgr = [sb(f"sgr{i}", [128, 256], F32) for i in range(NR)]
        cs = sb("cs", [128, 2, TB], F32)
        TL = sb("TL", [128, TB], BF16)
        SG = sb("SG", [128, TB], BF16)
        tn = ["sw", "asg", "LD", "L", "EL", "ENL", "EA", "EWC", "kk", "kkn", "km", "bv", "t1"]
        tm = {n: sb("t_" + n, [128, TB], F32) for n in tn}
        KSQ = sb("KSQ", [128, TB], BF16)
        RKb = sb("RKb", [128, TB], BF16)
        for b_ in hraw:
            k.memset("pool", b_, b_[:], 0.0)
        hp_ = []
        for hp in range(2):
            d_ = {}
            d_["AR"] = sb(f"AR{hp}", [128, NC, 256], BF16)
            for n in ("Bz", "Kz", "BWz", "KWz", "Vz", "Ynz"):
                d_[n] = sb(f"{n}{hp}", [128, NC, 128], BF16)
            for n in ("AR", "Bz", "Kz", "BWz", "KWz", "Vz", "Ynz"):
                k.memset("pool", d_[n], d_[n][:], 0.0)
            d_["WC"] = sb(f"WC{hp}", [128, NC], F32)
            d_["RV"] = sb(f"RV{hp}", [128, TB], F32)
            d_["G"] = sb(f"G{hp}", [128, TB], F32)
            d_["S32"] = sb(f"S32{hp}", [128, 64], F32)
            d_["Sbf"] = sb(f"Sbf{hp}", [128, 64], BF16)
            k.memset("dve", d_["S32"], d_["S32"][:], 0.0)
            k.memset("dve", d_["Sbf"], d_["Sbf"][:], 0.0)
            d_["ysq"] = sb(f"ysq{hp}", [128, TB], F32)
            d_["o1"] = sb(f"o1{hp}", [128, TB], F32)
            d_["YA"] = sb(f"YA{hp}", [128, TB], BF16)
            for n in ("s1", "s2", "mean", "msq", "var", "sd", "rstd", "nmr"):
                d_[n] = sb(f"{n}{hp}", [128, NC], F32)
            hp_.append(d_)
        qb = []
        for q in range(2 * NC):
            d_ = {}
            d_["MN"] = sb(f"MN{q}", [128, 512], BF16)
            d_["Q0"] = sb(f"Q0{q}", [128, 128], BF16)
            d_["PQ"] = [sb(f"PQ{q}_{i}", [128, 256], BF16) for i in range(2)]
            d_["Z"] = [sb(f"Z{q}_{i}", [128, 192], BF16) for i in range(2)]
            d_["Vst"] = sb(f"Vst{q}", [128, 64], BF16)
            d_["Az"] = sb(f"Az{q}", [128, 128], BF16)
            d_["Uv"] = sb(f"Uv{q}", [128, 64], F32)
            d_["BKWT"] = sb(f"BKWT{q}", [128, 256], BF16)
            d_["P5I"] = sb(f"P5I{q}", [128, 128], BF16)
            d_["UT"] = sb(f"UT{q}", [128, 64], BF16)
            qb.append(d_)
        qr = sb("qr", [128, 2, TB], BF16)
        qd = sb("qd", [128, 2, TB], BF16)
        kr = sb("kr", [128, 2, TB], BF16)
        ra = sb("ra", [128, TB], F32)
        rb = sb("rb", [128, TB], F32)
        rc = sb("rc", [128, TB], F32)
        rd_ = sb("rd", [128, TB], F32)
        STm = sb("STm", [128, 128], BF16)
        ktm = sb("ktm", [128, 256], BF16)
        R32 = sb("R32", [128, 512], F32)
        Rbf = sb("Rbf", [128, 512], BF16)
        k.memset("dve", R32, R32[:], 0.0)
        k.memset("dve", Rbf, Rbf[:], 0.0)
        rst6 = sb("rst6", [128, 6], F32)
        rmv = sb("rmv", [128, 2], F32)
        rsd = sb("rsd", [128, 1], F32)
        rrs = sb("rrs", [128, 1], F32)
        rnm = sb("rnm", [128, 1], F32)
        yb = sb("yb", [128, 256], F32)
        ybb = sb("ybb", [128, 256], BF16)
        YB = sb("YB", [128, 2, TB], BF16)

        msend = dr["ybuf"][yrow0:yrow0 + 512, :]
        rg128 = sb("rg128", [128, 1], F32)
        k.dma("sync", rg128[:], dr["rg128"], "c_g128", [], [rg128])

        for b in range(NB):
            t0 = b * TB
            jdst = 0
            tq0 = t0
            xtb = xtbs[b % 2]
            if b == 0:
                k.dma("pool", xtb[:], xT_v[:, :, 0:TB], "xtb0", [], [xtb])
            if b + 1 < NB:
                xn_ = xtbs[(b + 1) % 2]
                k.dma("pool", xn_[:], xT_v[:, :, t0 + TB:t0 + 2 * TB], f"xtb{(b + 1) % 2}", [], [xn_])
            k.dma("sync", cs[:, 0, :], dr["cos"][:, t0:t0 + TB], "cs", [], [cs])
            k.dma("sync", cs[:, 1, :], dr["sin"][:, t0:t0 + TB], "cs", [], [cs])
            for cc in range(12):
                if cc % 2 == 0:
                    ps = P.ps()
                half = ps[:, (cc % 2) * TB:(cc % 2) * TB + TB]
                for kk_ in range(16):
                    k.mm(half, wfm[:, kk_, cc * 128:(cc + 1) * 128], xtb[:, kk_, :], kk_ == 0,
                         kk_ == 15, [wfm, xtb], [ps])
                if cc < 8:
                    hr = hraw[cc]
                    k.cp("act", hr[:, 1:TB + 1], half, [ps], [hr])
                    k.act(h1[cc % 2][:], half, AF.Identity, [ps, omm], [h1[cc % 2]],
                          scale=omm[:, cc:cc + 1])
                    k.stt(Ff[cc][:], hr[:, 0:TB], pv[:, cc:cc + 1], h1[cc % 2][:], ALU.mult, ALU.add,
                          [hr, pv, h1[cc % 2]], [Ff[cc]])
                    k.cp("pool", hr[:, 0:1], hr[:, TB:TB + 1], [hr], [hr])
                else:
                    k.cp("act", QK[cc - 8][:], half, [ps], [QK[cc - 8]])
            P.mark('inproj_fm')
            for sub in range(NR):
                ps = P.ps()
                for kk_ in range(16):
                    k.mm(ps[:, 0:512], xtb[:, kk_, sub * 128:(sub + 1) * 128], wvg[:, kk_, :],
                         kk_ == 0, kk_ == 15, [xtb, wvg], [ps])
                k.cp("act", vtm[sub][:], ps[:, 0:256], [ps], [vtm[sub]])
                k.act(sgr[sub][:], ps[:, 256:512], AF.Silu, [ps], [sgr[sub]])
            P.mark('inproj')
            k.act(TL[0:64, :], Ff[6][0:64, :], AF.Tanh, [Ff[6]], [TL])
            k.cp("pool", TL[64:128, :], Ff[6][64:128, :], [Ff[6]], [TL])
            k.act(SG[:], Ff[7][:], AF.Sigmoid, [Ff[7]], [SG])
            P.mark('e0')
            for hp in range(2):
                h = hp_[hp]
                cs_ = slice(hp * 128, (hp + 1) * 128)
                Fr, Fk, Fv = Ff[hp], Ff[2 + hp], Ff[4 + hp]
                ps1 = P.ps()
                k.mm(ps1[:, 0:TB], wupw[:, cs_], TL[:], True, True, [wupw, TL], [ps1])
                k.mm(ps1[:, TB:2 * TB], wupa[:, cs_], TL[:], True, True, [wupa, TL], [ps1])
                P.mark(f'm1_{hp}')
                ps2 = P.ps()
                k.mm(ps2[:, 0:TB], gup[:, cs_], SG[:], True, True, [gup, SG], [ps2])
                P.mark(f'm2_{hp}')
                k.act(tm["sw"][:], ps1[:, 0:TB], AF.Sigmoid, [ps1, pv], [tm["sw"]], bias=pv[:, 8 + hp:9 + hp])
                k.act(tm["asg"][:], ps1[:, TB:2 * TB], AF.Sigmoid, [ps1, pv], [tm["asg"]],
                      bias=pv[:, 10 + hp:11 + hp])
                P.mark(f'm3_{hp}')
                k.cp("act", h["G"][:], ps2[:, 0:TB], [ps2], [h["G"]])
                P.mark(f'e1_{hp}')
                k.ts("dve", tm["LD"][:], tm["sw"][:], -math.exp(-0.5), None, ALU.mult, None,
                     [tm["sw"]], [tm["LD"]])
                for c in range(NC):
                    c_ = slice(c * CH, (c + 1) * CH)
                    P.op("dve", (lambda o_, d1: (lambda e: e.tensor_tensor_scan(
                        out=o_, data0=ones[:, 0:CH], data1=d1, initial=0.0, op0=ALU.mult,
                        op1=ALU.add)))(tm["L"][:, c_], tm["LD"][:, c_]), [ones, _k(tm["LD"])],
                        [tm["L"]])
                P.mark(f'e2_{hp}')
                k.act(tm["EL"][:], tm["L"][:], AF.Exp, [tm["L"]], [tm["EL"]])
                k.act(tm["ENL"][:], tm["L"][:], AF.Exp, [tm["L"]], [tm["ENL"]], scale=-1.0)
                k.tt("dve", tm["t1"][:], tm["L"][:], tm["LD"][:], ALU.subtract, [tm["L"], tm["LD"]],
                     [tm["t1"]])
                k.act(tm["EA"][:], tm["t1"][:], AF.Exp, [tm["t1"]], [tm["EA"]])
                for c in range(NC):
                    c_ = slice(c * CH, (c + 1) * CH)
                    k.act(tm["EWC"][:, c_], tm["L"][:, c_], AF.Exp, [tm["L"]], [tm["EWC"]],
                          scale=-1.0, bias=tm["L"][:, c * CH + CH - 1:c * CH + CH])
                    k.cp("pool", h["WC"][:, c:c + 1], tm["EL"][:, c * CH + CH - 1:c * CH + CH],
                         [tm["EL"]], [h["WC"]])
                P.mark(f'e3_{hp}')
                k.ts("dve", tm["kk"][:], Fk[:], pv[:, 12 + hp:13 + hp], None, ALU.mult, None,
                     [Fk, pv], [tm["kk"]])
                k.tt("pool", KSQ[:], tm["kk"][:], tm["kk"][:], ALU.mult, [tm["kk"]], [KSQ])
                k.mm(ps2[:, TB:2 * TB], onesbd[:], KSQ[:], True, True, [onesbd, KSQ], [ps2])
                P.mark(f'e4_{hp}')
                k.act(tm["t1"][:], ps2[:, TB:2 * TB], AF.Sqrt, [ps2], [tm["t1"]])
                k.ts("dve", tm["t1"][:], tm["t1"][:], 1e-12, None, ALU.max, None, [tm["t1"]], [tm["t1"]])
                P.op("dve", lambda e: e.reciprocal(out=tm["t1"][:], in_=tm["t1"][:]), [_k(tm["t1"])],
                     [_k(tm["t1"])])
                k.tt("dve", tm["kkn"][:], tm["kk"][:], tm["t1"][:], ALU.mult, [tm["kk"], tm["t1"]],
                     [tm["kkn"]])
                P.mark(f'e5_{hp}')
                k.ts("dve", tm["t1"][:], tm["asg"][:], pv[:, 14 + hp:15 + hp], omka[:, hp:hp + 1],
                     ALU.mult, ALU.add, [tm["asg"], pv, omka], [tm["t1"]])
                k.tt("dve", tm["km"][:], Fk[:], tm["t1"][:], ALU.mult, [Fk, tm["t1"]], [tm["km"]])
                k.tt("pool", tm["bv"][:], tm["kkn"][:], tm["asg"][:], ALU.mult, [tm["kkn"], tm["asg"]],
                     [tm["bv"]])
                P.mark(f'e6_{hp}')
                for hh in range(2):
                    rs = slice(hh * 64, (hh + 1) * 64)
                    cz = slice(hh * 64, (hh + 1) * 64)

                    def v3(buf):
                        return buf[rs, :].rearrange("p (c t) -> p c t", t=CH)

                    k.tt("dve", h["AR"][rs, :, 128 + hh * 64:128 + (hh + 1) * 64], v3(Fr), v3(tm["EL"]),
                         ALU.mult, [Fr, tm["EL"]], [h["AR"]])
                    k.stt(h["AR"][rs, :, cz], v3(tm["kkn"]), -1.0, v3(tm["EA"]), ALU.mult, ALU.mult,
                          [tm["kkn"], tm["EA"]], [h["AR"]])
                    k.tt("pool", h["Kz"][rs, :, cz], v3(tm["km"]), v3(tm["ENL"]), ALU.mult,
                         [tm["km"], tm["ENL"]], [h["Kz"]])
                    k.tt("pool", h["Bz"][rs, :, cz], v3(tm["bv"]), v3(tm["ENL"]), ALU.mult,
                         [tm["bv"], tm["ENL"]], [h["Bz"]])
                    k.tt("pool", h["KWz"][rs, :, cz], v3(tm["km"]), v3(tm["EWC"]), ALU.mult,
                         [tm["km"], tm["EWC"]], [h["KWz"]])
                    k.tt("dve", h["BWz"][rs, :, cz], v3(tm["bv"]), v3(tm["EWC"]), ALU.mult,
                         [tm["bv"], tm["EWC"]], [h["BWz"]])
                    k.cp("pool", h["Vz"][rs, :, cz], v3(Fv), [Fv], [h["Vz"]])
                P.mark(f'e7_{hp}')
                k.stt(RKb[:], Fr[:], pv[:, 16 + hp:17 + hp], tm["km"][:], ALU.mult, ALU.mult,
                      [Fr, pv, tm["km"]], [RKb])
                ps3 = P.ps()
                k.mm(ps3[:, 0:TB], onesbd[:], RKb[:], True, True, [onesbd, RKb], [ps3])
                k.tt("dve", h["RV"][:], ps3[:, 0:TB], Fv[:], ALU.mult, [ps3, Fv], [h["RV"]])
                P.mark(f'e8_{hp}')
            P.mark('elem')
            qs = [(hp, c) for hp in range(2) for c in range(NC)]
            for qi, (hp, c) in enumerate(qs):
                h, q = hp_[hp], qb[qi]
                psG = P.ps()
                k.mm(psG[:, 0:256], h["Bz"][:, c, :], h["AR"][:, c, :], True, True, [h["Bz"], h["AR"]], [psG])
                k.mm(psG[:, 256:512], h["Kz"][:, c, :], h["AR"][:, c, :], True, True, [h["Kz"], h["AR"]], [psG])
                k.tt("dve", q["MN"][:], psG[:, 0:512], mask512[:], ALU.mult, [psG, mask512], [q["MN"]])
                psH = P.ps()
                k.mm(psH[:, 0:128], h["AR"][:, c, 0:128], h["Bz"][:, c, :], True, True, [h["AR"], h["Bz"]], [psH])
                k.mm(psH[:, 128:192], h["Vz"][:, c, :], esel[:], True, True, [h["Vz"], esel], [psH])
                k.mm(psH[:, 192:320], h["AR"][:, c, 0:128], ident[:], True, True, [h["AR"], ident], [psH])
                k.tt("dve", q["Q0"][:], psH[:, 0:128], maskT[:], ALU.mult, [psH, maskT], [q["Q0"]])
                k.cp("act", q["Vst"][:], psH[:, 128:192], [psH], [q["Vst"]])
                k.cp("act", q["Z"][0][:, 0:128], psH[:, 192:320], [psH], [q["Z"][0]])
                psI = P.ps()
                k.mm(psI[:, 0:128], h["BWz"][:, c, :], ident[:], True, True, [h["BWz"], ident], [psI])
                k.mm(psI[:, 128:256], h["KWz"][:, c, :], ident[:], True, True, [h["KWz"], ident], [psI])
                k.cp("act", q["BKWT"][:], psI[:, 0:256], [psI], [q["BKWT"]])
            for qi, (hp, c) in enumerate(qs):
                q = qb[qi]
                psW = P.ps()
                k.mm(psW[:, 0:64], q["MN"][:, 256:384], q["Vst"][:], True, True, [q["MN"], q["Vst"]], [psW])
                k.cp("act", q["Z"][0][:, 128:192], psW[:, 0:64], [psW], [q["Z"][0]])
            for lv in range(5):
                for qi, (hp, c) in enumerate(qs):
                    q = qb[qi]
                    if lv == 0:
                        Pi, Qi, prd = q["MN"][:, 0:128], q["Q0"][:], [q["MN"], q["Q0"]]
                    else:
                        pq = q["PQ"][(lv - 1) % 2]
                        Pi, Qi, prd = pq[:, 0:128], pq[:, 128:256], [pq]
                    Zi, Zo = q["Z"][lv % 2], q["Z"][(lv + 1) % 2]
                    psZ = P.ps()
                    k.mm(psZ[:, 0:192], Pi, Zi[:, 0:192], True, True, prd + [Zi], [psZ])
                    k.mm(psZ[:, 192:320], Qi, Pi, True, True, prd, [psZ])
                    if lv < 4:
                        k.mm(psZ[:, 320:448], Pi, Qi, True, True, prd, [psZ])
                    k.tt("dve", Zo[:, 0:192], psZ[:, 0:192], Zi[:, 0:192], ALU.add, [psZ, Zi], [Zo])
                    if lv < 4:
                        k.cp("act", q["PQ"][lv % 2][:], psZ[:, 192:448], [psZ], [q["PQ"][lv % 2]])
                    else:
                        k.tt("dve", q["P5I"][:], psZ[:, 192:320], ident[:], ALU.add, [psZ, ident], [q["P5I"]])
            for qi, (hp, c) in enumerate(qs):
                q = qb[qi]
                Z5 = q["Z"][1]
                psF = P.ps()
                k.mm(psF[:, 0:128], Z5[:, 0:128], q["P5I"][:], True, True, [Z5, q["P5I"]], [psF])
                k.mm(psF[:, 128:192], q["P5I"][:], Z5[:, 128:192], True, True, [Z5, q["P5I"]], [psF])
                k.cp("act", q["Az"][:], psF[:, 0:128], [psF], [q["Az"]])
                k.cp("dve", q["Uv"][:], psF[:, 128:192], [psF], [q["Uv"]])
            P.mark('pre')
            ypsb = [psum[6], psum[7]]
            for c in range(NC):
                for hp in range(2):
                    h, q = hp_[hp], qb[hp * NC + c]
                    yps = ypsb[hp]
                    psS = P.ps()
                    k.mm(psS[:, 0:64], q["Az"][:], h["Sbf"][:], True, True, [q["Az"], h["Sbf"]], [psS])
                    k.tt("dve", q["UT"][:], psS[:, 0:64], q["Uv"][:], ALU.add, [psS, q["Uv"]], [q["UT"]])
                    k.mm(psS[:, 64:128], q["BKWT"][:, 0:128], q["UT"][:], True, False, [q["BKWT"], q["UT"]], [psS])
                    k.mm(psS[:, 64:128], q["BKWT"][:, 128:256], q["Vst"][:], False, True, [q["BKWT"], q["Vst"]], [psS])
                    yc = yps[:, c * 64:(c + 1) * 64]
                    k.mm(yc, h["AR"][:, c, 128:256], h["Sbf"][:], True, False, [h["AR"], h["Sbf"]], [yps])
                    k.mm(yc, q["MN"][:, 128:256], q["UT"][:], False, False, [q["MN"], q["UT"]], [yps])
                    k.mm(yc, q["MN"][:, 384:512], q["Vst"][:], False, True, [q["MN"], q["Vst"]], [yps])
                    k.stt(h["S32"][:], h["S32"][:], h["WC"][:, c:c + 1], psS[:, 64:128], ALU.mult, ALU.add,
                          [h["S32"], h["WC"], psS], [h["S32"]])
                    k.cp("act", h["Sbf"][:], h["S32"][:], [h["S32"]], [h["Sbf"]])
            P.mark('seq')
            for hp in range(2):
                h = hp_[hp]
                yps = ypsb[hp]
                y3 = yps[:, 0:TB].rearrange("p (c v) -> p c v", v=64)
                k.act(h["ysq"][:], yps[:, 0:TB], AF.Square, [yps], [h["ysq"]])
                P.op("dve", (lambda o_, i_: (lambda e: e.tensor_reduce(out=o_, in_=i_, axis=AX.X, op=ALU.add)))(
                    h["s1"][:], y3), [_k(yps)], [_k(h["s1"])])
                P.op("dve", (lambda o_, i_: (lambda e: e.tensor_reduce(out=o_, in_=i_, axis=AX.X, op=ALU.add)))(
                    h["s2"][:], h["ysq"][:].rearrange("p (c v) -> p c v", v=64)), [_k(h["ysq"])], [_k(h["s2"])])
                k.ts("dve", h["mean"][:], h["s1"][:], 1.0 / 64, None, ALU.mult, None, [h["s1"]], [h["mean"]])
                k.tt("dve", h["msq"][:], h["mean"][:], h["mean"][:], ALU.mult, [h["mean"]], [h["msq"]])
                k.stt(h["var"][:], h["s2"][:], 1.0 / 64, h["msq"][:], ALU.mult, ALU.subtract,
                      [h["s2"], h["msq"]], [h["var"]])
                k.ts("dve", h["var"][:], h["var"][:], 64e-5, None, ALU.add, None, [h["var"]], [h["var"]])
                k.act(h["sd"][:], h["var"][:], AF.Sqrt, [h["var"]], [h["sd"]])
                P.op("dve", (lambda o_, i_: (lambda e: e.reciprocal(out=o_, in_=i_)))(h["rstd"][:], h["sd"][:]),
                     [_k(h["sd"])], [_k(h["rstd"])])
                k.stt(h["nmr"][:], h["mean"][:], -1.0, h["rstd"][:], ALU.mult, ALU.mult,
                      [h["mean"], h["rstd"]], [h["nmr"]])
                for c in range(NC):
                    for hh in range(2):
                        rs = slice(hh * 64, (hh + 1) * 64)
                        k.act(h["Ynz"][rs, c, hh * 64:(hh + 1) * 64], yps[rs, c * 64:(c + 1) * 64], AF.Identity,
                              [yps, h["rstd"], h["nmr"]], [h["Ynz"]], scale=h["rstd"][rs, c:c + 1],
                              bias=h["nmr"][rs, c:c + 1])
                psY = P.ps()
                for c in range(NC):
                    k.mm(psY[:, c * 64:(c + 1) * 64], h["Ynz"][:, c, :], esel[:], True, True, [h["Ynz"], esel], [psY])
                k.act(h["o1"][:], psY[:, 0:TB], AF.Identity, [psY, pv], [h["o1"]], scale=pv[:, 18 + hp:19 + hp],
                      bias=pv[:, 20 + hp:21 + hp])
                k.tt("dve", h["o1"][:], h["o1"][:], h["RV"][:], ALU.add, [h["o1"], h["RV"]], [h["o1"]])
                k.tt("dve", h["YA"][:], h["o1"][:], h["G"][:], ALU.mult, [h["o1"], h["G"]], [h["YA"]])
                k.dma("sync", msend[jdst * 512 + hp * 128:jdst * 512 + (hp + 1) * 128, tq0:tq0 + TB], h["YA"][:],
                      f"ya{hp}", [h["YA"]], ["ybuf"])
            P.mark('rout')
            cosb, sinb = cs[:, 0, :], cs[:, 1, :]
            for (src0, src1, dst) in ((QK[0], QK[1], qr), (QK[2], QK[3], kr)):
                k.tt("dve", ra[:], src0[:], cosb, ALU.mult, [src0, cs], [ra])
                k.tt("pool", rb[:], src1[:], sinb, ALU.mult, [src1, cs], [rb])
                k.tt("dve", dst[:, 0, :], ra[:], rb[:], ALU.subtract, [ra, rb], [dst])
                k.tt("pool", rc[:], src0[:], sinb, ALU.mult, [src0, cs], [rc])
                k.tt("dve", rd_[:], src1[:], cosb, ALU.mult, [src1, cs], [rd_])
                k.tt("pool", dst[:, 1, :], rc[:], rd_[:], ALU.add, [rc, rd_], [dst])
            for dc in range(2):
                k.tt("pool", qd[:, dc, :], qr[:, dc, :], rqdec[:], ALU.mult, [qr, rqdec], [qd])
            for ch in range(NR):
                c_ = slice(ch * 128, (ch + 1) * 128)
                psA = P.ps()
                k.mm(psA[:, 0:128], kr[:, 0, c_], qr[:, 0, c_], True, False, [kr, qr], [psA])
                k.mm(psA[:, 0:128], kr[:, 1, c_], qr[:, 1, c_], False, True, [kr, qr], [psA])
                k.mm(psA[:, 128:256], kr[:, 0, c_], ident[:], True, True, [kr, ident], [psA])
                k.mm(psA[:, 256:384], kr[:, 1, c_], ident[:], True, True, [kr, ident], [psA])
                k.tt("dve", STm[:], psA[:, 0:128], rmask[:], ALU.mult, [psA, rmask], [STm])
                k.act(ktm[:], psA[:, 128:384], AF.Identity, [psA, rkdec], [ktm], scale=rkdec[:, 0:1])
                psB = P.ps()
                k.mm(psB[:, 0:256], STm[:], vtm[ch][:], True, False, [STm, vtm[ch]], [psB])
                k.mm(psB[:, 0:256], qd[:, 0, c_], Rbf[:, 0:256], False, False, [qd, Rbf], [psB])
                k.mm(psB[:, 0:256], qd[:, 1, c_], Rbf[:, 256:512], False, True, [qd, Rbf], [psB])
                psC = P.ps()
                k.mm(psC[:, 0:256], ktm[:, 0:128], vtm[ch][:], True, True, [ktm, vtm[ch]], [psC])
                k.mm(psC[:, 256:512], ktm[:, 128:256], vtm[ch][:], True, True, [ktm, vtm[ch]], [psC])
                k.stt(R32[:], R32[:], rg128[:, 0:1], psC[:, 0:512], ALU.mult, ALU.add, [R32, psC, rg128], [R32])
                k.cp("act", Rbf[:], R32[:], [R32], [Rbf])
                P.op("dve", (lambda i_: (lambda e: e.bn_stats(out=rst6[:], in_=i_)))(psB[:, 0:256]), [_k(psB)], [_k(rst6)])
                P.op("dve", lambda e: e.bn_aggr(out=rmv[:], in_=rst6[:]), [_k(rst6)], [_k(rmv)])
                k.ts("dve", rsd[:], rmv[:, 1:2], 1e-6, None, ALU.add, None, [rmv], [rsd])
                k.act(rsd[:], rsd[:], AF.Sqrt, [rsd], [rsd])
                P.op("dve", lambda e: e.reciprocal(out=rrs[:], in_=rsd[:]), [_k(rsd)], [_k(rrs)])
                k.stt(rnm[:], rmv[:, 0:1], -1.0, rrs[:], ALU.mult, ALU.mult, [rmv, rrs], [rnm])
                k.act(yb[:], psB[:, 0:256], AF.Identity, [psB, rrs, rnm], [yb], scale=rrs[:, 0:1], bias=rnm[:, 0:1])
                k.tt("dve", yb[:], yb[:], retgn[:], ALU.mult, [yb, retgn], [yb])
                k.tt("pool", ybb[:], yb[:], sgr[ch][:], ALU.mult, [yb, sgr[ch]], [ybb])
                psD = P.ps()
                k.mm(psD[:, 0:128], ybb[:, 0:128], ident[:], True, True, [ybb, ident], [psD])
                k.mm(psD[:, 128:256], ybb[:, 128:256], ident[:], True, True, [ybb, ident], [psD])
                k.cp("act", YB[:, :, c_], psD[:, 0:256].rearrange("p (e c) -> p e c", c=128), [psD], [YB])
            k.dma("sync", msend[jdst * 512 + 256:jdst * 512 + 512, tq0:tq0 + TB].rearrange("(e p) t -> p e t", p=128),
                  YB[:], "yb", [YB], ["ybuf"])
        P.emit(st, tag)


def phase2(nc, psum, T, dr, mixcol=0):
    TQ = T // 4
    NT = TQ // TT
    NW = 4
    with ExitStack() as st:
        P = Prog(nc, psum)
        P.final_eng = "sync"
        k = K(P, nc, st)
        sb = k.sb
        hT = sb("hT", [128, 64, TT], BF16)
        x1T = sb("x1T", [128, 16, TT], BF16)
        xr = [sb(f"xr{i}", [128, D], F32) for i in range(4)]
        xb = sb("xb", [128, D], BF16)
        lnt = sb("lnt", [128, 2, D], F32)
        ring = [sb(f"wr{i}", [128, 8, 512], BF16) for i in range(NW)]
        rtmp = [sb(f"rt{i}", [128, TT], F32) for i in range(2)]
        st6 = sb("st6", [128, 4, 6], F32)
        mv = sb("mv", [128, 2], F32)
        sd = sb("sd", [128, 1], F32)
        rstd = sb("rstd", [128, 1], F32)
        nmr = sb("nmr", [128, 1], F32)
        ident = sb("ident2", [128, 128], BF16)
        k.dma("pool", ident[:], dr["ident"], "c_id", [], [ident])
        wn = [0]

        def wpiece(w, r0, c0):
            s = ring[wn[0] % NW]
            wn[0] += 1
            k.dma("pool", s[:], w[r0:r0 + 1024, c0:c0 + 512].rearrange("(k p) c -> p k c", p=128), "w_" + s.key,
                  [], [s])
            return s

        def layer_norm(sub, which):
            x_ = xr[sub]
            for g in range(4):
                P.op("dve", (lambda o_, i_: (lambda e: e.bn_stats(out=o_, in_=i_)))(
                    st6[:, g, :], x_[:, g * 512:(g + 1) * 512]), [_k(x_)], [_k(st6)])
            P.op("dve", lambda e: e.bn_aggr(out=mv[:], in_=st6[:].rearrange("p g s -> p (g s)")), [_k(st6)], [_k(mv)])
            k.ts("dve", sd[:], mv[:, 1:2], 1e-5, None, ALU.add, None, [mv], [sd])
            k.act(sd[:], sd[:], AF.Sqrt, [sd], [sd])
            P.op("dve", lambda e: e.reciprocal(out=rstd[:], in_=sd[:]), [_k(sd)], [_k(rstd)])
            k.stt(nmr[:], mv[:, 0:1], -1.0, rstd[:], ALU.mult, ALU.mult, [mv, rstd], [nmr])
            k.act(x_[:], x_[:], AF.Identity, [x_, rstd, nmr], [x_], scale=rstd[:, 0:1], bias=nmr[:, 0:1])
            k.tt("dve", x_[:], x_[:], lnt[:, 0, :], ALU.mult, [x_, lnt], [x_])
            k.tt("dve", x_[:], x_[:], lnt[:, 1, :], ALU.add, [x_, lnt], [x_])

        mrecv = dr["mix"]
        for t in range(NT):
            tok0 = t * TT
            for hg in range(4):
                r0_ = hg * 512
                k.dma("sync", hT[:, hg * 4:(hg + 1) * 4, :],
                      mrecv[r0_:r0_ + 512, mixcol + tok0:mixcol + tok0 + TT].rearrange("(c p) t -> p c t", p=128),
                      "mixld", ["ybuf"], [hT])
            for sub in range(4):
                k.dma("sync", xr[sub][:], dr["xres"][tok0 + sub * 128:tok0 + (sub + 1) * 128, :], f"xr{sub}", [],
                      [xr[sub]])
            k.dma("sync", lnt[:], dr["ln"][:, 0:2, :], "ln", [], [lnt])
            for j in range(4):
                banks = [P.ps() for _ in range(4)]
                for kh in range(2):
                    pc = wpiece(dr["wo"], kh * 1024, j * 512)
                    for sub in range(4):
                        for k8 in range(8):
                            k.mm(banks[sub][:, 0:512], hT[:, kh * 8 + k8, sub * 128:(sub + 1) * 128], pc[:, k8, :],
                                 kh == 0 and k8 == 0, kh == 1 and k8 == 7, [hT, pc], [banks[sub]])
                for sub in range(4):
                    xs_ = xr[sub][:, j * 512:(j + 1) * 512]
                    k.stt(xs_, xs_, ALPHA, banks[sub][:, 0:512], ALU.mult, ALU.add, [xr[sub], banks[sub]], [xr[sub]])
            for sub in range(4):
                layer_norm(sub, 0)
                k.cp("act", xb[:], xr[sub][:], [xr[sub]], [xb])
                for g in range(4):
                    ps = P.ps()
                    for kk_ in range(4):
                        k.mm(ps[:, kk_ * 128:(kk_ + 1) * 128], xb[:, (g * 4 + kk_) * 128:(g * 4 + kk_ + 1) * 128],
                             ident[:], True, True, [xb, ident], [ps])
                    k.cp("act", x1T[:, g * 4:(g + 1) * 4, sub * 128:(sub + 1) * 128],
                         ps[:, 0:512].rearrange("p (k t) -> p k t", t=128), [ps], [x1T])
            k.dma("sync", lnt[:], dr["ln"][:, 2:4, :], "ln", [], [lnt])
            for cg in range(16):
                banks = [P.ps() for _ in range(4)]
                for kh in range(2):
                    pc = wpiece(dr["wup"], kh * 1024, cg * 512)
                    for fc in range(4):
                        for k8 in range(8):
                            k.mm(banks[fc][:, 0:512], pc[:, k8, fc * 128:(fc + 1) * 128], x1T[:, kh * 8 + k8, :],
                                 kh == 0 and k8 == 0, kh == 1 and k8 == 7, [x1T, pc], [banks[fc]])
                for fc in range(4):
                    rt = rtmp[fc % 2]
                    k.act(rt[:], banks[fc][:, 0:512], AF.Relu, [banks[fc]], [rt])
                    k.tt("dve", hT[:, cg * 4 + fc, :], rt[:], rt[:], ALU.mult, [rt], [hT])
            for j in range(4):
                banks = [P.ps() for _ in range(4)]
                for kg in range(8):
                    pc = wpiece(dr["wdown"], kg * 1024, j * 512)
                    for sub in range(4):
                        for k8 in range(8):
                            k.mm(banks[sub][:, 0:512], hT[:, kg * 8 + k8, sub * 128:(sub + 1) * 128], pc[:, k8, :],
                                 kg == 0 and k8 == 0, kg == 7 and k8 == 7, [hT, pc], [banks[sub]])
                for sub in range(4):
                    xs_ = xr[sub][:, j * 512:(j + 1) * 512]
                    k.stt(xs_, xs_, ALPHA, banks[sub][:, 0:512], ALU.mult, ALU.add, [xr[sub], banks[sub]], [xr[sub]])
            for sub in range(4):
                layer_norm(sub, 1)
                k.dma("sync", dr["out"][tok0 + sub * 128:tok0 + (sub + 1) * 128, :], xr[sub][:], f"xo{sub}",
                      [xr[sub]], ["out"])
        P.emit(st, "b")


P1_INPUTS = [("xT", None), ("wfm", [D, 1536]), ("wvg", [D, 512]), ("pvec", [128, 24]), ("wupw", [64, 256]),
             ("wupa", [64, 256]), ("gup", [128, 256]), ("mask512", [128, 512]), ("maskT", [128, 128]),
             ("ident", [128, 128]), ("esel", [128, 64]), ("onesbd", [128, 128]), ("retgn", [128, 256]),
             ("rmask", [128, 128]), ("rkdec", [128, 1]), ("rqdec", [128, 256]), ("rg128", [128, 1]),
             ("cos", None), ("sin", None)]


def build_p1(T):
    nc = bass.Bass("TRN2", target_bir_lowering=False)
    dr = {}
    for name, shape in P1_INPUTS:
        if name == "xT":
            shape = [D, T]
        elif shape is None:
            shape = [128, T]
        dr[name] = nc.dram_tensor(name, list(shape), F32, kind="ExternalInput").ap()
    dr["ybuf"] = nc.dram_tensor("ybuf", [512, T], BF16, kind="ExternalOutput").ap()
    with ExitStack() as st0:
        psum = [Buf(st0.enter_context(nc.psum_tensor(f"ps{i}", [128, 512], F32)), f"ps{i}") for i in range(8)]
        phase1(nc, psum, T, dr)
    return nc


def build_p2(T):
    TQ = T // 4
    nc = bass.Bass("TRN2", target_bir_lowering=False)
    dr = {}
    for name, shape, dt in (("mix", [2048, TQ], BF16), ("ident", [128, 128], F32), ("xres", [TQ, D], F32),
                            ("wo", [D, D], F32), ("wup", [D, DFF], F32), ("wdown", [DFF, D], F32),
                            ("ln", [128, 4, D], F32)):
        dr[name] = nc.dram_tensor(name, list(shape), dt, kind="ExternalInput").ap()
    dr["out"] = nc.dram_tensor("out", [TQ, D], F32, kind="ExternalOutput").ap()
    with ExitStack() as st0:
        psum = [Buf(st0.enter_context(nc.psum_tensor(f"ps{i}", [128, 512], F32)), f"ps{i}") for i in range(8)]
        phase2(nc, psum, T, dr)
    return nc


def _consts():
    hh = np.arange(128) // 64
    ii = np.arange(128) % 64
    same = hh[:, None] == hh[None, :]
    strict = same & (ii[:, None] < ii[None, :])
    incl = same & (ii[:, None] <= ii[None, :])
    mask512 = np.concatenate([strict, incl, strict, incl], axis=1).astype(np.float32)
    maskT = (same & (ii[:, None] > ii[None, :])).astype(np.float32)
    ident = np.eye(128, dtype=np.float32)
    esel = (ii[:, None] == np.arange(64)[None, :]).astype(np.float32)
    onesbd = same.astype(np.float32)
    return mask512, maskT, ident, esel, onesbd


def make_in_maps(inputs, T):
    f32 = np.float32
    x = np.asarray(inputs["x"], f32)
    w_in = np.asarray(inputs["w_in"], f32)[0]
    mu = np.asarray(inputs["mu_shift"], f32)[0]
    g = lambda n: np.asarray(inputs[n], f32)[0]
    w0, a0, k_k, k_a = g("w0"), g("a0"), g("k_k"), g("k_a")
    r_k = g("r_k").reshape(-1)
    gnw, gnb, rgn = g("rwkv_gn_w"), g("rwkv_gn_b"), g("ret_gn_w")
    wlu, alu_, glu = g("w_lora_up"), g("a_lora_up"), g("g_lora_up")
    w_o, w_up, w_down = g("w_o"), g("w_up"), g("w_down")
    lnp = np.stack([g("ln1_w"), g("ln1_b"), g("ln2_w"), g("ln2_b")], 0)
    ln_t = np.ascontiguousarray(np.broadcast_to(lnp[None], (128, 4, D)))
    mask512, maskT, ident, esel, onesbd = _consts()
    TQ = T // 4
    T8 = T // 8
    pos = np.arange(T, dtype=f32)
    inv_freq = (1.0 / (f32(10000.0) ** np.linspace(0.0, 1.0, 128, dtype=f32))).astype(f32)
    theta = (pos[None, :] * inv_freq[:, None]).astype(f32)
    cos_t = np.cos(theta.astype(np.float64)).astype(f32)
    sin_t = np.sin(theta.astype(np.float64)).astype(f32)
    perm = []
    for hg in range(4):
        perm += list(range(hg * 256, hg * 256 + 256)) + list(range(1024 + hg * 256, 1024 + hg * 256 + 256))
    wo_p = np.ascontiguousarray(w_o[np.array(perm)])
    xT = [np.ascontiguousarray(x[b].T) for b in range(x.shape[0])]
    idx = np.arange(128, dtype=np.float64)
    maps = []
    gam128 = None
    for b in range(2):
        for hg in range(4):
            ch = slice(hg * 256, hg * 256 + 256)
            rb = 3328
            ev = np.arange(0, 256, 2)
            od = np.arange(1, 256, 2)
            cols = np.concatenate([
                np.arange(hg * 256, hg * 256 + 256), 1024 + np.arange(hg * 256, hg * 256 + 256),
                2048 + np.arange(hg * 256, hg * 256 + 256), np.arange(3072, 3328),
                rb + hg * 256 + ev, rb + hg * 256 + od, rb + 1024 + hg * 256 + ev, rb + 1024 + hg * 256 + od])
            wfm = np.ascontiguousarray(w_in[:, cols])
            vg = np.concatenate([rb + 2048 + np.arange(hg * 256, hg * 256 + 256),
                                 rb + 3072 + np.arange(hg * 256, hg * 256 + 256)])
            wvg = np.ascontiguousarray(w_in[:, vg])
            pvec = np.zeros((128, 24), f32)
            pvec[:, 0:8] = mu[cols[:1024]].reshape(8, 128).T
            for i, v in enumerate((w0, a0, k_k, k_a, r_k, gnw, gnb)):
                pvec[:, 8 + 2 * i:10 + 2 * i] = v[ch].reshape(2, 128).T
            lg = math.log(1.0 - 2.0 ** (-5.0 - hg))
            rmask = np.where(idx[None, :] >= idx[:, None], np.exp((idx[None, :] - idx[:, None]) * lg), 0.0) / 16.0
            rkdec = (np.exp((127.0 - idx) * lg) / 16.0)[:, None]
            rq = np.exp((idx + 1.0) * lg)
            rqdec = np.broadcast_to(np.tile(rq, 2)[None, :], (128, 256))
            m = {
                "xT": xT[b], "wfm": wfm, "wvg": wvg, "pvec": pvec,
                "wupw": np.ascontiguousarray(wlu[:, ch]), "wupa": np.ascontiguousarray(alu_[:, ch]),
                "gup": np.ascontiguousarray(glu[:, ch]),
                "mask512": mask512, "maskT": maskT, "ident": ident, "esel": esel, "onesbd": onesbd,
                "retgn": np.ascontiguousarray(np.broadcast_to(rgn[ch][None, :], (128, 256))),
                "rmask": np.ascontiguousarray(rmask.astype(f32)), "rkdec": np.ascontiguousarray(rkdec.astype(f32)),
                "rqdec": np.ascontiguousarray(rqdec.astype(f32)),
                "rg128": np.full((128, 1), math.exp(128.0 * lg), f32), "cos": cos_t, "sin": sin_t,
                "xres": np.ascontiguousarray(x[b, hg * TQ:(hg + 1) * TQ, :]),
                "wo": wo_p, "wup": w_up, "wdown": w_down, "ln": ln_t,
            }
            maps.append(m)
    return maps


def kernel(**inputs):
    x = np.asarray(inputs["x"])
    B, T, _ = x.shape
    TQ = T // 4
    maps = make_in_maps(inputs, T)
    p1_names = [n for n, _ in P1_INPUTS]
    nc1 = build_p1(T)
    res1 = run_bass_kernel_spmd(nc1, [{n: m[n] for n in p1_names} for m in maps], core_ids=list(range(8)))
    ybufs = [res1.results[r]["ybuf"] for r in range(8)]
    nc2 = build_p2(T)
    maps2 = []
    for b in range(2):
        for j in range(4):
            m = maps[b * 4 + j]
            mix = np.concatenate([ybufs[b * 4 + hg][:, j * TQ:(j + 1) * TQ] for hg in range(4)], 0)
            maps2.append({"mix": np.ascontiguousarray(mix), "ident": m["ident"], "xres": m["xres"], "wo": m["wo"],
                          "wup": m["wup"], "wdown": m["wdown"], "ln": m["ln"]})
    res2 = run_bass_kernel_spmd(nc2, maps2, core_ids=list(range(8)))
    out = np.empty((B, T, D), np.float32)
    for b in range(2):
        for j in range(4):
            out[b, j * TQ:(j + 1) * TQ, :] = res2.results[b * 4 + j]["out"]
    return out
```
